# Optimizing a Trainium2 kernel written in Bass

```python
import jax, jax.numpy as jnp
from jax import lax
import numpy as np

D_MODEL = 4096
BATCH = 2
SEQ = 4096
DEPTH = 1

CHUNK = 64
Q_BLOCK = 128
D_FF = 11008
N_MOD = 9
CONV_GROUPS = 16
CONV_GROUP_DIM = 128
CONV_WIDTH = CONV_GROUPS * CONV_GROUP_DIM
CONV_K = 3
N_HEADS = 16
QK_NOPE = 128
QK_ROPE = 64
QK_HEAD = QK_NOPE + QK_ROPE
V_HEAD = 128
MLA_WIDTH = N_HEADS * V_HEAD
Q_LORA = 1024
KV_LORA = 512
ROPE_THETA = 10000.0
N_BRANCH = 2
EPS = 1e-6
MAX_POS_OFFSET = 4096
IN_SPLITS = (CONV_WIDTH, CONV_WIDTH, CONV_WIDTH, Q_LORA, KV_LORA, QK_ROPE, N_BRANCH * D_MODEL)

kernel_name = "chunk_causal_hybrid_conv_mla_macaron_adaln"


def _split_points(sizes):
    pts, acc = [], 0
    for s in sizes[:-1]:
        acc += s
        pts.append(acc)
    return pts


def rmsnorm(x, g):
    xf = x.astype(jnp.float32)
    y = xf * lax.rsqrt(jnp.mean(xf * xf, axis=-1, keepdims=True) + EPS)
    return (y * g.astype(jnp.float32)).astype(x.dtype)


def modulate(h, shift, scale):
    return h * (1.0 + scale[:, None, :]) + shift[:, None, :]


def swiglu(h, w_gate, w_up, w_down):
    return (jax.nn.silu(h @ w_gate) * (h @ w_up)) @ w_down


def rope_tables(positions, dtype):
    inv_freq = ROPE_THETA ** (-jnp.arange(0, QK_ROPE, 2, dtype=jnp.float32) / QK_ROPE)
    ang = positions.astype(jnp.float32)[..., None] * inv_freq
    return jnp.cos(ang).astype(dtype), jnp.sin(ang).astype(dtype)


def apply_rope(x, cos, sin):
    x1, x2 = jnp.split(x, 2, axis=-1)
    c, s = cos[:, :, None, :], sin[:, :, None, :]
    return jnp.concatenate([x1 * c - x2 * s, x2 * c + x1 * s], axis=-1)


def chunk_causal_attention(q, k, v):
    b, s, h, dq = q.shape
    n_blocks = s // Q_BLOCK
    scale = dq ** -0.5
    key_chunk = jnp.arange(s) // CHUNK
    qb = q.reshape(b, n_blocks, Q_BLOCK, h, dq).transpose(1, 0, 2, 3, 4)

    def one_block(args):
        q_blk, blk = args
        scores = jnp.einsum('bqhd,bkhd->bhqk', q_blk, k).astype(jnp.float32) * scale
        q_chunk = (blk * Q_BLOCK + jnp.arange(Q_BLOCK)) // CHUNK
        mask = key_chunk[None, :] <= q_chunk[:, None]
        scores = jnp.where(mask[None, None], scores, -jnp.inf)
        p = jax.nn.softmax(scores, axis=-1).astype(v.dtype)
        return jnp.einsum('bhqk,bkhd->bqhd', p, v)

    out = lax.map(one_block, (qb, jnp.arange(n_blocks)))
    return out.transpose(1, 0, 2, 3, 4).reshape(b, s, h, v.shape[-1])


def short_conv(u, b_gate, c_gate, conv_w, conv_b):
    s = u.shape[1]
    z = c_gate * u
    zp = jnp.pad(z, ((0, 0), (CONV_K - 1, 0), (0, 0)))
    conv = sum(conv_w[j] * zp[:, j:j + s, :] for j in range(CONV_K)) + conv_b
    return b_gate * conv


def hybrid_mixer(h, cos, sin, w_in, b_gate, conv_w, conv_b, w_conv_out,
                 q_lat_norm, kv_lat_norm, w_uq, w_ukv, q_norm, k_norm, w_mla_out, w_out):
    b, s, _ = h.shape
    proj = h @ w_in
    u, g_b, g_c, q_lat, kv_lat, k_pe, gate_logits = jnp.split(proj, _split_points(IN_SPLITS), axis=-1)

    y_conv = short_conv(u, g_b, g_c, conv_w, conv_b) @ w_conv_out

    q = (rmsnorm(q_lat, q_lat_norm) @ w_uq).reshape(b, s, N_HEADS, QK_HEAD)
    kv = (rmsnorm(kv_lat, kv_lat_norm) @ w_ukv).reshape(b, s, N_HEADS, QK_NOPE + V_HEAD)
    k_nope, v = kv[..., :QK_NOPE], kv[..., QK_NOPE:]
    k_pe = jnp.broadcast_to(k_pe[:, :, None, :], (b, s, N_HEADS, QK_ROPE))
    k = jnp.concatenate([k_nope, k_pe], axis=-1)
    q = rmsnorm(q, q_norm)
    k = rmsnorm(k, k_norm)
    q = jnp.concatenate([q[..., :QK_NOPE], apply_rope(q[..., QK_NOPE:], cos, sin)], axis=-1)
    k = jnp.concatenate([k[..., :QK_NOPE], apply_rope(k[..., QK_NOPE:], cos, sin)], axis=-1)
    attn = chunk_causal_attention(q, k, v).reshape(b, s, MLA_WIDTH)
    y_mla = attn @ w_mla_out

    gates = jax.nn.sigmoid(gate_logits + b_gate).reshape(b, s, N_BRANCH, D_MODEL)
    merged = gates[:, :, 0, :] * y_conv + gates[:, :, 1, :] * y_mla
    return merged @ w_out


def setup_inputs(seed: int = 0) -> dict:
    key = jax.random.key(seed)
    ks = jax.random.split(key, 32)
    L, D = DEPTH, D_MODEL
    in_width = sum(IN_SPLITS)

    def dense(k, shape, fan_in, gain=1.0):
        return gain * fan_in ** -0.5 * jax.random.normal(k, shape, jnp.float32)

    def gain_vec(k, shape):
        return 1.0 + 0.05 * jax.random.normal(k, shape, jnp.float32)

    def small(k, shape):
        return 0.01 * jax.random.normal(k, shape, jnp.float32)

    offset = jax.random.randint(ks[2], (BATCH,), 0, MAX_POS_OFFSET // CHUNK) * CHUNK
    positions = (offset[:, None] + jnp.arange(SEQ)[None, :]).astype(jnp.int32)
    return {
        "x": jax.random.normal(ks[0], (BATCH, SEQ, D), jnp.float32),
        "c": jax.random.normal(ks[1], (BATCH, D), jnp.float32),
        "positions": positions,
        "w_ada": dense(ks[3], (L, D, N_MOD * D), D, 0.2),
        "b_ada": small(ks[4], (L, N_MOD * D)),
        "norm_ffn1": gain_vec(ks[5], (L, D)),
        "ffn1_w_gate": dense(ks[6], (L, D, D_FF), D),
        "ffn1_w_up": dense(ks[7], (L, D, D_FF), D),
        "ffn1_w_down": dense(ks[8], (L, D_FF, D), D_FF),
        "norm_mix": gain_vec(ks[9], (L, D)),
        "w_in": dense(ks[10], (L, D, in_width), D),
        "b_gate": small(ks[11], (L, N_BRANCH * D)),
        "conv_w": dense(ks[12], (L, CONV_K, CONV_WIDTH), CONV_K),
        "conv_b": small(ks[13], (L, CONV_WIDTH)),
        "w_conv_out": dense(ks[14], (L, CONV_WIDTH, D), CONV_WIDTH),
        "q_lat_norm": gain_vec(ks[15], (L, Q_LORA)),
        "kv_lat_norm": gain_vec(ks[16], (L, KV_LORA)),
        "w_uq": dense(ks[17], (L, Q_LORA, N_HEADS * QK_HEAD), Q_LORA),
        "w_ukv": dense(ks[18], (L, KV_LORA, N_HEADS * (QK_NOPE + V_HEAD)), KV_LORA),
        "q_norm": gain_vec(ks[19], (L, QK_HEAD)),
        "k_norm": gain_vec(ks[20], (L, QK_HEAD)),
        "w_mla_out": dense(ks[21], (L, MLA_WIDTH, D), MLA_WIDTH),
        "w_out": dense(ks[22], (L, D, D), D),
        "norm_ffn2": gain_vec(ks[23], (L, D)),
        "ffn2_w_gate": dense(ks[24], (L, D, D_FF), D),
        "ffn2_w_up": dense(ks[25], (L, D, D_FF), D),
        "ffn2_w_down": dense(ks[26], (L, D_FF, D), D_FF),
    }


def reference(x, c, positions, w_ada, b_ada, norm_ffn1, ffn1_w_gate, ffn1_w_up, ffn1_w_down,
              norm_mix, w_in, b_gate, conv_w, conv_b, w_conv_out, q_lat_norm, kv_lat_norm,
              w_uq, w_ukv, q_norm, k_norm, w_mla_out, w_out, norm_ffn2,
              ffn2_w_gate, ffn2_w_up, ffn2_w_down):
    cos, sin = rope_tables(positions, x.dtype)
    c_act = jax.nn.silu(c)
    for l in range(DEPTH):
        mod = c_act @ w_ada[l] + b_ada[l]
        sh1, sc1, g1, sh2, sc2, g2, sh3, sc3, g3 = jnp.split(mod, N_MOD, axis=-1)
        h = modulate(rmsnorm(x, norm_ffn1[l]), sh1, sc1)
        x = x + 0.5 * g1[:, None, :] * swiglu(h, ffn1_w_gate[l], ffn1_w_up[l], ffn1_w_down[l])
        h = modulate(rmsnorm(x, norm_mix[l]), sh2, sc2)
        y = hybrid_mixer(h, cos, sin, w_in[l], b_gate[l], conv_w[l], conv_b[l], w_conv_out[l],
                         q_lat_norm[l], kv_lat_norm[l], w_uq[l], w_ukv[l], q_norm[l], k_norm[l],
                         w_mla_out[l], w_out[l])
        x = x + g2[:, None, :] * y
        h = modulate(rmsnorm(x, norm_ffn2[l]), sh3, sc3)
        x = x + 0.5 * g3[:, None, :] * swiglu(h, ffn2_w_gate[l], ffn2_w_up[l], ffn2_w_down[l])
    return x
```

```python
import math
from contextlib import ExitStack

import numpy as np
import concourse.bass as bass
import concourse.mybir as mybir
from concourse.bass_utils import run_bass_kernel_spmd

F32 = mybir.dt.float32
BF16 = mybir.dt.bfloat16
I32 = mybir.dt.int32
ALU = mybir.AluOpType
AF = mybir.ActivationFunctionType
EPS = 1e-6
NEG = -30000.0


class Cfg:
    def __init__(self, D=4096, F=11008, S=4096, B=2, H=16, QL=1024, KVL=512, CW=2048, G=4, NCORES=8):
        self.D, self.F, self.S, self.B, self.H, self.QL, self.KVL, self.CW, self.G = D, F, S, B, H, QL, KVL, CW, G
        self.NCORES = NCORES
        self.R = NCORES // B
        self.T = S // self.R
        self.TT = min(512, self.T)
        self.NH = self.T // self.TT
        self.MC, self.FC, self.CC, self.QC, self.KC = D // 128, F // 128, CW // 128, QL // 128, KVL // 128
        self.HG = min(4, H)
        self.NHG = H // self.HG
        base, rem = self.FC // G, self.FC % G
        self.FG = [base + (1 if i < rem else 0) for i in range(G)]
        self.FB = [sum(self.FG[:i]) for i in range(G)]
        self.KT = self.T // 128
        self.DT = self.TT // 128
        self.NWIN = self.KC + 1 + 3 * self.CC + self.QC + 2 * self.MC
        self.NCC = -(-9 * self.MC // self.R)
        cols = {}
        n = 0
        for name, w in [("gain1", self.MC), ("gain2", self.MC), ("gain3", self.MC), ("bada", self.R * self.NCC),
                        ("bg0", self.MC), ("bg1", self.MC), ("cw0", self.CC), ("cw1", self.CC), ("cw2", self.CC),
                        ("cb", self.CC), ("qln", self.QC), ("kvn", self.KC), ("gq_n", 1), ("gq_r", 1), ("gq_rs", 1),
                        ("gk_n", 1), ("gk_r", 1), ("gk_rs", 1), ("freq", 1), ("sgn", 1), ("halfpi", 1)]:
            cols[name] = (n, w)
            n += w
        self.vcols, self.NV = cols, n


class Buf:
    __slots__ = ("name", "w", "r", "excl")

    def __init__(self, name, excl=False):
        self.name, self.w, self.r, self.excl = name, None, {}, excl


class Prog:
    NDS = 8

    def __init__(self, nc, es):
        self.nc = nc
        self.E = {"pe": nc.tensor, "dve": nc.vector, "act": nc.scalar, "pool": nc.gpsimd, "sp": nc.sync}
        self.sems = {}
        for k in self.E:
            self.sems[k] = es.enter_context(nc.semaphore("c_" + k))
        self.cnt = {k: 0 for k in self.E}
        self.last = {k: None for k in self.E}
        self.waited = {k: {} for k in self.E}
        self.dq = {}
        for q in ("sp", "act", "pool"):
            keys = []
            for i in range(self.NDS):
                key = "d_%s%d" % (q, i)
                self.sems[key] = es.enter_context(nc.semaphore(key))
                keys.append(key)
            self.dq[q] = {"keys": keys, "rr": 0, "tgt": [0] * self.NDS}
        self.ccsem = es.enter_context(nc.semaphore("ccsem"))
        self.sems["cc"] = self.ccsem
        self.cccnt = 0

    def _flush(self, e):
        if self.last[e] is not None:
            self.last[e].then_inc(self.sems[e], 1)
            self.cnt[e] += 1
            self.last[e] = None

    def _wait(self, e, key, val):
        if key in self.E and val > self.cnt[key]:
            self._flush(key)
            assert val <= self.cnt[key]
        if self.waited[e].get(key, 0) >= val:
            return
        self.E[e].wait_ge(self.sems[key], val)
        self.waited[e][key] = val

    def _deps(self, e, reads, writes):
        deps = {}

        def add(ev):
            if ev is None:
                return
            k, v = ev
            if deps.get(k, 0) < v:
                deps[k] = v
        for b in reads:
            add(b.w)
            if b.excl:
                for k, v in b.r.items():
                    if k != e:
                        add((k, v))
        for b in writes:
            add(b.w)
            for k, v in b.r.items():
                add((k, v))
        for k, v in deps.items():
            if k == e and e == "pe":
                continue
            self._wait(e, k, v)

    def _mark(self, ev, reads, writes):
        k, v = ev
        for b in reads:
            if b.r.get(k, 0) < v:
                b.r[k] = v
        for b in writes:
            b.w = ev
            b.r = {}

    def op(self, e, fn, reads=(), writes=()):
        self._deps(e, reads, writes)
        inst = fn(self.E[e])
        self.last[e] = inst
        self._mark((e, self.cnt[e] + 1), reads, writes)
        if e != "pe":
            self._flush(e)

    def dma(self, q, out, in_, reads=(), writes=()):
        self._deps(q, reads, writes)
        d = self.dq[q]
        j = d["rr"]
        d["rr"] = (j + 1) % self.NDS
        key = d["keys"][j]
        if d["tgt"][j] > 0:
            self._wait(q, key, d["tgt"][j])
        d["tgt"][j] += 16
        self.E[q].dma_start(out=out, in_=in_).then_inc(self.sems[key], 16)
        self._mark((key, d["tgt"][j]), reads, writes)

    def collective(self, kind, groups, in_ap, out_ap, reads, writes, inc):
        self._deps("pool", reads, writes)
        self.cccnt += inc
        self.nc.gpsimd.collective_compute(kind, ALU.bypass, replica_groups=groups,
                                          ins=[in_ap], outs=[out_ap]).then_inc(self.ccsem, inc)
        self._mark(("cc", self.cccnt), reads, writes)

    def barrier(self):
        for e in self.E:
            self._flush(e)
        for e in self.E:
            for k in self.E:
                if k != e and self.cnt[k] > 0:
                    self._wait(e, k, self.cnt[k])
            for q in self.dq.values():
                for key, t in zip(q["keys"], q["tgt"]):
                    if t > 0:
                        self._wait(e, key, t)


_UID = [0]


class Ring:
    def __init__(self, es, nc, name, n, shape, dt):
        _UID[0] += 1
        self.t = [es.enter_context(nc.sbuf_tensor("r%d_%s%d" % (_UID[0], name, i), list(shape), dt)) for i in range(n)]
        self.b = [Buf("%s%d" % (name, i)) for i in range(n)]
        self.i = 0

    def next(self):
        j = self.i
        self.i = (j + 1) % len(self.t)
        return self.t[j], self.b[j]


def build_program(c, cc_inc=1, stop=None):
    nc = bass.Bass("TRN2", target_bir_lowering=False)
    D, T, TT, NH, MC, FC, CC, QC, KC, H, R = c.D, c.T, c.TT, c.NH, c.MC, c.FC, c.CC, c.QC, c.KC, c.H, c.R
    HG, NHG, KT, DT, G = c.HG, c.NHG, c.KT, c.DT, c.G
    KH = MC // 2

    def din(name, shape, dt=F32):
        return nc.dram_tensor(name, list(shape), dt, kind="ExternalInput").ap()

    xT = din("xT", [MC, 128, T])
    cT = din("cT", [128, MC])
    posb = din("posb", [64, T], I32)
    maskb = din("maskb", [128, max(R - 1, 1)])
    sel = din("sel", [128, R])
    vecs_d = din("vecs", [128, c.NV])
    wada = din("wada", [c.NCC, 128, MC * 128])
    wgu = [din("wgu%d" % l, [FC * 2, 128, KH * 2 * 128]) for l in (1, 2)]
    wdn = [[din("wdn%d_%d" % (l, g), [MC // 2, 128, c.FG[g] * 256]) for g in range(G)] for l in (1, 2)]
    win = din("win", [c.NWIN, 128, MC * 128])
    wco = din("wco", [MC, 128, CC * 128])
    wmo = din("wmo", [MC, 128, H * 128])
    wo = din("wo", [MC, 128, MC * 128])
    wuq = din("wuq", [H, 128, QC * 256])
    wuk = din("wuk", [H, 128, KC * 128])
    wuv = din("wuv", [NHG, 128, KC * HG * 128])
    out = nc.dram_tensor("out", [MC, 128, T], F32, kind="ExternalOutput").ap()

    xw = nc.dram_tensor("xw", [MC, 128, T], F32).ap()
    mgs = nc.dram_tensor("mgs", [MC, 128, T], BF16).ap()
    sds = nc.dram_tensor("sds", [CC, 128, T], BF16).ap()
    payK = [nc.dram_tensor("payK%d" % h, [192, T], BF16) for h in range(H)]
    agK = [nc.dram_tensor("agK%d" % h, [R * 192, T], BF16) for h in range(H)]
    payV = [nc.dram_tensor("payV%d" % h, [T, 128], BF16) for h in range(H)]
    agV = [nc.dram_tensor("agV%d" % h, [R * T, 128], BF16) for h in range(H)]
    payZ = nc.dram_tensor("payZ", [128, CC * 2], F32)
    agZ = nc.dram_tensor("agZ", [R * 128, CC * 2], F32)
    groups = [[b * R + r for r in range(R)] for b in range(c.B)]

    es = ExitStack()
    with es:
        es.enter_context(nc.allow_low_precision("bf16 matmul operands, fp32 accumulate"))
        P = Prog(nc, es)

        def sb(st, name, shape, dt):
            _UID[0] += 1
            return st.enter_context(nc.sbuf_tensor("s%d_%s" % (_UID[0], name), list(shape), dt))

        banks = [es.enter_context(nc.psum_tensor("bank%d" % i, [128, 512], F32)) for i in range(8)]
        bb = [Buf("bank%d" % i, excl=True) for i in range(8)]

        vecs = sb(es, "vecs", [128, c.NV], F32)
        vb = Buf("vecs")
        modc = sb(es, "modc", [128, c.R * c.NCC], F32)
        modb = Buf("modc")
        der = sb(es, "der", [128, 5 * MC], F32)
        derb = Buf("der")
        ones = sb(es, "ones", [128, 128], BF16)
        onesb = Buf("ones")
        cactb = Buf("cact")
        rstd = sb(es, "rstd", [128, T], F32)
        rstdb = Buf("rstd")
        hbuf = sb(es, "hbuf", [128, MC, T], BF16)
        hb = [Buf("h%d" % m) for m in range(MC)]
        xdb = [Buf("xd%d" % m) for m in range(MC)]

        def V(name, j=0, w=1, p=128):
            o = c.vcols[name][0]
            return vecs[0:p, o + j:o + j + w]

        epsc = sb(es, "epsc", [128, 1], F32)
        epsb = Buf("epsc")
        P.op("dve", lambda e: e.memset(epsc[:, :], EPS), (), (epsb,))
        P.dma("sp", vecs[:, :], vecs_d, (), (vb,))
        P.op("dve", lambda e: e.memset(ones[:, :], 1.0), (), (onesb,))

        def mm(outap, lhsT, rhs, start, stop, reads, writes):
            P.op("pe", lambda e: e.matmul(outap, lhsT, rhs, start=start, stop=stop), reads, writes)

        NCC = c.NCC
        payM = nc.dram_tensor("payM", [128, NCC], F32)
        agM = nc.dram_tensor("agM", [R * 128, NCC], F32)
        payMb, agMb = Buf("payM"), Buf("agM")
        def phase0():
            with ExitStack() as ph:
                ctile = sb(ph, "ctile", [128, MC], F32)
                ctb = Buf("ctile")
                cact = sb(ph, "cact", [128, MC], BF16)
                P.dma("sp", ctile[:, :], cT, (), (ctb,))
                P.op("act", lambda e: e.activation(cact[:, :], ctile[:, :], AF.Silu), (ctb,), (cactb,))
                ring = Ring(ph, nc, "wada", 4, [128, MC, 128], BF16)
                mb = bb[0]
                for cc in range(NCC):
                    wt, wb = ring.next()
                    P.dma("pool", wt[:, :, :], wada[cc].rearrange("p (k n) -> p k n", n=128), (), (wb,))
                    for kc in range(MC):
                        mm(banks[0][:, cc:cc + 1], wt[:, kc, :], cact[:, kc:kc + 1], kc == 0, kc == MC - 1,
                           (wb, cactb), (mb,))
                mst_ = sb(ph, "modst", [128, NCC], F32)
                mstb = Buf("modst")
                P.op("dve", lambda e: e.tensor_copy(mst_[:, :], banks[0][:, 0:NCC]), (mb,), (mstb,))
                P.dma("sp", payM.ap(), mst_[:, :], (mstb,), (payMb,))
                P.collective("AllGather", groups, payM.ap().opt(), agM.ap().opt(), (payMb,), (agMb,), cc_inc)
                mall = sb(ph, "mall", [128, R, NCC], F32)
                mallb = Buf("mall")
                P.dma("sp", mall[:, :, :], agM.ap().rearrange("(r p) c -> p r c", p=128), (agMb,), (mallb,))
                o = c.vcols["bada"][0]
                P.op("dve", lambda e: e.tensor_tensor(modc[:, :], mall[:, :, :].rearrange("p r c -> p (r c)"),
                                                     vecs[:, o:o + R * NCC], ALU.add), (mallb, vb), (modb,))
                for i, (gn, scj) in enumerate([("gain1", 1), ("gain2", 4), ("gain3", 7)]):
                    go = c.vcols[gn][0]
                    P.op("dve", lambda e, i=i, go=go, scj=scj: e.scalar_tensor_tensor(
                        der[:, i * MC:(i + 1) * MC], modc[:, scj * MC:(scj + 1) * MC], 1.0, vecs[:, go:go + MC],
                        ALU.add, ALU.mult), (modb, vb), (derb,))
                for i, gj in [(3, 2), (4, 8)]:
                    P.op("dve", lambda e, i=i, gj=gj: e.tensor_scalar(
                        der[:, i * MC:(i + 1) * MC], modc[:, gj * MC:(gj + 1) * MC], 0.5, None, ALU.mult),
                        (modb,), (derb,))
                P.barrier()

        def mod(j, m):
            return modc[:, j * MC + m:j * MC + m + 1]

        def dcol(i, m):
            return der[:, i * MC + m:i * MC + m + 1]

        def rstd_from(ss_banks, n, dst, dstb, p=128):
            for hh in range(NH):
                P.op("act", lambda e, hh=hh: e.activation(
                    dst[0:p, hh * TT:(hh + 1) * TT], banks[ss_banks[hh]][0:p, 0:TT], AF.Ln, bias=epsc[0:p, :], scale=1.0 / n),
                    (bb[ss_banks[hh]], epsb), (dstb,))
            P.op("act", lambda e: e.activation(dst[0:p, :], dst[0:p, :], AF.Exp, scale=-0.5), (dstb,), (dstb,))

        def norm_phase(src, a_i, sh_j, mid=None):
            with ExitStack() as ph:
                xs = Ring(ph, nc, "nxs", 3, [128, T], F32)
                sq = Ring(ph, nc, "nsq", 2, [128, T], BF16)
                tm = Ring(ph, nc, "ntm", 2, [128, T], F32)
                for m in range(MC):
                    xt, xb = xs.next()
                    P.dma("sp", xt[:, :], src[m], (xdb[m],), (xb,))
                    st, sb_ = sq.next()
                    P.op("act", lambda e, xt=xt, st=st: e.activation(st[:, :], xt[:, :], AF.Square), (xb,), (sb_,))
                    for hh in range(NH):
                        mm(banks[hh][:, 0:TT], ones[:, :], st[:, hh * TT:(hh + 1) * TT], m == 0, m == MC - 1,
                           (onesb, sb_), (bb[hh],))
                rstd_from(list(range(NH)), D, rstd, rstdb)
                if mid is not None:
                    mid()
                for m in range(MC):
                    xt, xb = xs.next()
                    P.dma("sp", xt[:, :], src[m], (xdb[m],), (xb,))
                    tt, tb = tm.next()
                    P.op("dve", lambda e, xt=xt, tt=tt: e.tensor_tensor(tt[:, :], xt[:, :], rstd[:, :], ALU.mult),
                         (xb, rstdb), (tb,))
                    P.op("act", lambda e, tt=tt, m=m: e.activation(hbuf[:, m, :], tt[:, :], AF.Identity,
                                                                   bias=mod(sh_j, m), scale=dcol(a_i, m)),
                         (tb, modb, derb), (hb[m],))
                P.barrier()

        def resid_evac(ph_rings, bank_ids, m, scale_ap, scale_bufs, src, dst):
            xin, xo = ph_rings
            xt, xb = xin.next()
            P.dma("sp", xt[:, :], src[m], (xdb[m],), (xb,))
            ot, ob = xo.next()
            for hh in range(NH):
                P.op("dve", lambda e, hh=hh: e.scalar_tensor_tensor(
                    ot[:, hh * TT:(hh + 1) * TT], banks[bank_ids[hh]][:, 0:TT], scale_ap,
                    xt[:, hh * TT:(hh + 1) * TT], ALU.mult, ALU.add),
                    (bb[bank_ids[hh]], xb) + tuple(scale_bufs), (ob,))
            P.dma("act", dst[m], ot[:, :], (ob,), (xdb[m],))

        def ffn_phase(l, hg_i, src0, dst_last):
            with ExitStack() as ph:
                FGM = max(c.FG)
                act = sb(ph, "ffact", [128, FGM, T], BF16)
                actb = [Buf("act%d" % i) for i in range(FGM)]
                gur = Ring(ph, nc, "gur", 4, [128, KH, 2, 128], BF16)
                dnr = Ring(ph, nc, "dnr", 3, [128, FGM, 256], BF16)
                sgr = Ring(ph, nc, "sgr", 2, [128, TT], F32)
                xin = Ring(ph, nc, "fxin", 2, [128, T], F32)
                xo = Ring(ph, nc, "fxo", 2, [128, T], F32)
                bset = 0
                for g in range(G):
                    FGg = c.FG[g]
                    for fcl in range(FGg):
                        fc = c.FB[g] + fcl
                        bs = [bset * 4 + i for i in range(4)]
                        bset ^= 1
                        for kh in range(2):
                            wt, wb = gur.next()
                            P.dma("pool", wt[:, :, :, :],
                                  wgu[l][fc * 2 + kh].rearrange("p (k u n) -> p k u n", u=2, n=128), (), (wb,))
                            for kcl in range(KH):
                                kc = kh * KH + kcl
                                for u in range(2):
                                    for hh in range(NH):
                                        bi = bs[u * NH + hh]
                                        mm(banks[bi][:, 0:TT], wt[:, kcl, u, :], hbuf[:, kc, hh * TT:(hh + 1) * TT],
                                           kc == 0, kc == MC - 1, (wb, hb[kc]), (bb[bi],))
                        for hh in range(NH):
                            st, sb_ = sgr.next()
                            gi, ui = bs[hh], bs[NH + hh]
                            P.op("act", lambda e, st=st, gi=gi: e.activation(st[:, :], banks[gi][:, 0:TT], AF.Silu),
                                 (bb[gi],), (sb_,))
                            P.op("dve", lambda e, st=st, ui=ui, hh=hh, fcl=fcl: e.tensor_tensor(
                                act[:, fcl, hh * TT:(hh + 1) * TT], st[:, :], banks[ui][:, 0:TT], ALU.mult),
                                (sb_, bb[ui]), (actb[fcl],))
                    src = src0 if g == 0 else xw
                    dst = dst_last if g == G - 1 else xw
                    for mg in range(MC // 2):
                        wt, wb = dnr.next()
                        P.dma("pool", wt[:, 0:FGg, :], wdn[l][g][mg].rearrange("p (f n) -> p f n", n=256), (), (wb,))
                        bs = [bset * 4 + i for i in range(4)]
                        bset ^= 1
                        for fcl in range(FGg):
                            for mcl in range(2):
                                for hh in range(NH):
                                    bi = bs[mcl * NH + hh]
                                    mm(banks[bi][:, 0:TT], wt[:, fcl, mcl * 128:(mcl + 1) * 128],
                                       act[:, fcl, hh * TT:(hh + 1) * TT], fcl == 0, fcl == FGg - 1,
                                       (wb, actb[fcl]), (bb[bi],))
                        for mcl in range(2):
                            m = mg * 2 + mcl
                            resid_evac((xin, xo), [bs[mcl * NH + hh] for hh in range(NH)], m, dcol(hg_i, m), (derb,),
                                       src, dst)
                P.barrier()

        def proj(wt, wb, wsl, nk, srcf, srcb, bank_ids, M=128):
            for kc in range(nk):
                for hh in range(NH):
                    bi = bank_ids[hh]
                    mm(banks[bi][0:M, 0:TT], wt[:, kc, wsl], srcf(kc, hh), kc == 0, kc == nk - 1,
                       (wb, srcb[kc]), (bb[bi],))

        def hsrc(kc, hh):
            return hbuf[:, kc, hh * TT:(hh + 1) * TT]

        def mixer_phase():
            mx = ExitStack()
            with mx:
                Cq = sb(mx, "Cq", [64, T], F32)
                Sq = sb(mx, "Sq", [64, T], F32)
                Cqb, Sqb = Buf("Cq"), Buf("Sq")
                zprev = sb(mx, "zprev", [128, CC, 2], F32)
                gb2 = sb(mx, "gb2", [128, CC, 2], F32)
                t2 = sb(mx, "t2", [128, CC, 2], F32)
                zl = sb(mx, "zl", [128, CC, 2], F32)
                gb2b, t2b, zlb, zpb = Buf("gb2"), Buf("t2"), Buf("zl"), Buf("zprev")
                mskt = sb(mx, "mskt", [128, max(R - 1, 1)], F32)
                selt = sb(mx, "selt", [128, R], F32)
                mskb_, selb = Buf("mskt"), Buf("selt")
                P.dma("sp", mskt[:, :], maskb, (), (mskb_,))
                P.dma("sp", selt[:, :], sel, (), (selb,))
                payZb, agZb = (Buf(n) for n in ("payZ", "agZ"))
                agKb = [Buf("agK%d" % h) for h in range(H)]
                agVb = [Buf("agV%d" % h) for h in range(H)]
                payKn = [Buf("payKn%d" % h) for h in range(H)]
                payKr = [Buf("payKr%d" % h) for h in range(H)]
                payVh = [[Buf("payV%d_%d" % (h_, k_)) for k_ in range(KT)] for h_ in range(H)]
                sdb = [Buf("sds%d" % j) for j in range(CC)]
                mgb = [Buf("mgs%d" % m) for m in range(MC)]
                wring_st = ExitStack()
                mx.enter_context(wring_st)
                wr = Ring(wring_st, nc, "winr", 3, [128, MC, 128], BF16)
                widx = [0]

                def next_w():
                    wt, wb = wr.next()
                    i = widx[0]
                    widx[0] += 1
                    P.dma("pool", wt[:, :, :], win[i].rearrange("p (k n) -> p k n", n=128), (), (wb,))
                    return wt, wb

                with ExitStack() as ph:
                    Ck = sb(ph, "Ck", [64, T], F32)
                    Sk = sb(ph, "Sk", [64, T], F32)
                    Ckb, Skb = Buf("Ck"), Buf("Sk")
                    with ExitStack() as p2:
                        posi = sb(p2, "posi", [64, T], I32)
                        posf = sb(p2, "posf", [64, T], F32)
                        ang = sb(p2, "ang", [64, T], F32)
                        cosT = sb(p2, "cosT", [64, T], F32)
                        sinT = sb(p2, "sinT", [64, T], F32)
                        pib, pfb, anb, cob, sib = (Buf(n) for n in ("posi", "posf", "ang", "cosT", "sinT"))
                        P.dma("sp", posi[:, :], posb, (), (pib,))
                        P.op("dve", lambda e: e.tensor_copy(posf[:, :], posi[:, :]), (pib,), (pfb,))
                        P.op("dve", lambda e: e.tensor_scalar(posf[:, :], posf[:, :], V("freq", p=64), None,
                                                             ALU.mult), (pfb, vb), (pfb,))
                        C1 = 6.28125
                        C2 = 2 * math.pi - C1
                        P.op("dve", lambda e: e.tensor_scalar(ang[:, :], posf[:, :], 1.0 / (2 * math.pi), None, ALU.mult),
                             (pfb,), (anb,))
                        P.op("dve", lambda e: e.tensor_copy(posi[:, :], ang[:, :]), (anb, pib), (pib,))
                        P.op("dve", lambda e: e.tensor_copy(ang[:, :], posi[:, :]), (pib,), (anb,))
                        P.op("dve", lambda e: e.scalar_tensor_tensor(posf[:, :], ang[:, :], -C1, posf[:, :], ALU.mult, ALU.add),
                             (anb, pfb), (pfb,))
                        P.op("dve", lambda e: e.scalar_tensor_tensor(posf[:, :], ang[:, :], -C2, posf[:, :], ALU.mult, ALU.add),
                             (anb, pfb), (pfb,))
                        P.op("dve", lambda e: e.tensor_scalar(ang[:, :], posf[:, :], math.pi, None, ALU.is_gt), (pfb,), (anb,))
                        P.op("dve", lambda e: e.scalar_tensor_tensor(posf[:, :], ang[:, :], -2 * math.pi, posf[:, :], ALU.mult, ALU.add),
                             (anb, pfb), (pfb,))
                        P.op("dve", lambda e: e.tensor_scalar(ang[:, :], posf[:, :], -math.pi, None, ALU.is_lt), (pfb,), (anb,))
                        P.op("dve", lambda e: e.scalar_tensor_tensor(posf[:, :], ang[:, :], 2 * math.pi, posf[:, :], ALU.mult, ALU.add),
                             (anb, pfb), (pfb,))
                        P.op("act", lambda e: e.activation(sinT[:, :], posf[:, :], AF.Sin, scale=V("sgn", p=64)),
                             (pfb, vb), (sib,))
                        P.op("act", lambda e: e.activation(ang[:, :], posf[:, :], AF.Abs), (pfb,), (anb,))
                        P.op("act", lambda e: e.activation(cosT[:, :], ang[:, :], AF.Sin, bias=V("halfpi", p=64), scale=-1.0),
                             (anb, vb), (cob,))
                        for (dst, dstb_, srcT, srcb_, gname) in ((Cq, Cqb, cosT, cob, "gq_r"), (Sq, Sqb, sinT, sib, "gq_rs"),
                                                                 (Ck, Ckb, cosT, cob, "gk_r"), (Sk, Skb, sinT, sib, "gk_rs")):
                            P.op("dve", lambda e, dst=dst, srcT=srcT, gname=gname: e.tensor_scalar(
                                dst[:, :], srcT[:, :], V(gname, p=64), None, ALU.mult), (srcb_, vb), (dstb_,))
                        P.barrier()
                    if stop == "Ma_tab":
                        return "stopped"
                    kraw = sb(ph, "kvraw", [128, KC, T], F32)
                    kvn = sb(ph, "kvn", [128, KC, T], BF16)
                    krawb = [Buf("kvraw%d" % i) for i in range(KC)]
                    kvnb = [Buf("kvn%d" % i) for i in range(KC)]
                    sqr = Ring(ph, nc, "sqr", 2, [128, T], BF16)
                    rr_ = sb(ph, "rrope", [64, T], F32)
                    rrb = Buf("rrope")
                    tr1 = sb(ph, "tr1", [64, T], F32)
                    tr1b = Buf("tr1")
                    sqpe = sb(ph, "sqpe", [64, T], BF16)
                    sqpeb = Buf("sqpe")
                    rsr = Ring(ph, nc, "rsr", 2, [128, T], F32)
                    knr = Ring(ph, nc, "knr", 2, [128, T], BF16)
                    krr = Ring(ph, nc, "krr", 2, [64, T], BF16)
                    vsr = Ring(ph, nc, "vsr", 3, [128, HG * 128], BF16)
                    wukr = Ring(ph, nc, "wukr", 2, [128, KC, 128], BF16)
                    wuvr = Ring(ph, nc, "wuvr", 2, [128, KC, HG * 128], BF16)
                    ssb = [6, 7][:NH] if NH <= 2 else None
                    for kc in range(KC):
                        wt, wb = next_w()
                        bs = [(kc % 2) * NH + hh for hh in range(NH)]
                        proj(wt, wb, slice(0, 128), MC, hsrc, hb, bs)
                        st, sb_ = sqr.next()
                        for hh in range(NH):
                            P.op("act", lambda e, st=st, hh=hh, bi=bs[hh]: e.activation(
                                st[:, hh * TT:(hh + 1) * TT], banks[bi][:, 0:TT], AF.Square), (bb[bs[hh]],), (sb_,))
                            P.op("dve", lambda e, hh=hh, bi=bs[hh], kc=kc: e.tensor_copy(
                                kraw[:, kc, hh * TT:(hh + 1) * TT], banks[bi][:, 0:TT]), (bb[bs[hh]],), (krawb[kc],))
                        for hh in range(NH):
                            mm(banks[ssb[hh]][:, 0:TT], ones[:, :], st[:, hh * TT:(hh + 1) * TT], kc == 0, kc == KC - 1,
                               (onesb, sb_), (bb[ssb[hh]],))
                    rstd_from(ssb, c.KVL, rstd, rstdb)
                    for kc in range(KC):
                        P.op("dve", lambda e, kc=kc: e.scalar_tensor_tensor(
                            kvn[:, kc, :], kraw[:, kc, :], V("kvn", kc), rstd[:, :], ALU.mult, ALU.mult),
                            (krawb[kc], vb, rstdb), (kvnb[kc],))
                    if stop == "Ma_lat":
                        P.barrier()
                        return "stopped"
                    wt, wb = next_w()
                    b_pe = list(range(NH))
                    b_sw = list(range(NH, 2 * NH))
                    proj(wt, wb, slice(0, 64), MC, hsrc, hb, b_pe, M=64)
                    proj(wt, wb, slice(64, 128), MC, hsrc, hb, b_sw, M=64)
                    for hh in range(NH):
                        hs = slice(hh * TT, (hh + 1) * TT)
                        P.op("act", lambda e, hh=hh, hs=hs: e.activation(sqpe[:, hs], banks[b_pe[hh]][0:64, 0:TT], AF.Square),
                             (bb[b_pe[hh]],), (sqpeb,))
                        P.op("dve", lambda e, hh=hh, hs=hs: e.tensor_tensor(rr_[:, hs], banks[b_pe[hh]][0:64, 0:TT], Ck[:, hs], ALU.mult),
                             (bb[b_pe[hh]], Ckb), (rrb,))
                        P.op("dve", lambda e, hh=hh, hs=hs: e.tensor_tensor(tr1[:, hs], banks[b_sw[hh]][0:64, 0:TT], Sk[:, hs], ALU.mult),
                             (bb[b_sw[hh]], Skb), (tr1b,))
                    P.op("dve", lambda e: e.tensor_tensor(rr_[:, :], rr_[:, :], tr1[:, :], ALU.add), (rrb, tr1b), (rrb,))
                    if stop == "Ma_pe":
                        P.barrier()
                        return "stopped"
                    for h in range(H):
                        wt, wb = wukr.next()
                        P.dma("pool", wt[:, :, :], wuk[h].rearrange("p (k n) -> p k n", n=128), (), (wb,))
                        bs = [(h % 2) * 2 * NH + hh for hh in range(NH)]
                        sbk = [(h % 2) * 2 * NH + NH + hh for hh in range(NH)]
                        proj(wt, wb, slice(0, 128), KC, lambda kc, hh: kvn[:, kc, hh * TT:(hh + 1) * TT], kvnb, bs)
                        st, sb_ = sqr.next()
                        for hh in range(NH):
                            P.op("act", lambda e, st=st, hh=hh, bi=bs[hh]: e.activation(
                                st[:, hh * TT:(hh + 1) * TT], banks[bi][:, 0:TT], AF.Square), (bb[bs[hh]],), (sb_,))
                        for hh in range(NH):
                            hs = slice(hh * TT, (hh + 1) * TT)
                            mm(banks[sbk[hh]][:, 0:TT], ones[:, :], st[:, hs], True, False, (onesb, sb_), (bb[sbk[hh]],))
                            mm(banks[sbk[hh]][:, 0:TT], ones[0:64, :], sqpe[:, hs], False, True, (onesb, sqpeb), (bb[sbk[hh]],))
                        rt, rb = rsr.next()
                        rstd_from(sbk, 192, rt, rb)
                        kt_, kb_ = knr.next()
                        for hh in range(NH):
                            hs = slice(hh * TT, (hh + 1) * TT)
                            P.op("dve", lambda e, hh=hh, hs=hs, kt_=kt_, rt=rt, bi=bs[hh]: e.scalar_tensor_tensor(
                                kt_[:, hs], banks[bi][:, 0:TT], V("gk_n"), rt[:, hs], ALU.mult, ALU.mult),
                                (bb[bs[hh]], vb, rb), (kb_,))
                        P.dma("sp", payK[h].ap()[0:128, :], kt_[:, :], (kb_,), (payKn[h],))
                        kr_, krb_ = krr.next()
                        P.op("dve", lambda e, kr_=kr_, rt=rt: e.tensor_tensor(kr_[:, :], rr_[:, :], rt[0:64, :], ALU.mult),
                             (rrb, rb), (krb_,))
                        P.dma("sp", payK[h].ap()[128:192, :], kr_[:, :], (krb_,), (payKr[h],))
                    if stop == "Ma_k":
                        P.barrier()
                        return "stopped"
                    vcnt = 0
                    for hg in range(NHG):
                        wt, wb = wuvr.next()
                        P.dma("pool", wt[:, :, :], wuv[hg].rearrange("p (k n) -> p k n", n=HG * 128), (), (wb,))
                        for kt in range(KT):
                            bi = vcnt % 8
                            vcnt += 1
                            for kc in range(KC):
                                mm(banks[bi][:, 0:HG * 128], kvn[:, kc, kt * 128:(kt + 1) * 128], wt[:, kc, :],
                                   kc == 0, kc == KC - 1, (kvnb[kc], wb), (bb[bi],))
                            vt, vtb = vsr.next()
                            P.op("act", lambda e, vt=vt, bi=bi: e.activation(vt[:, :], banks[bi][:, 0:HG * 128], AF.Copy),
                                 (bb[bi],), (vtb,))
                            for j_ in range(HG):
                                h_ = hg * HG + j_
                                P.dma("sp", payV[h_].ap()[kt * 128:(kt + 1) * 128, :], vt[:, j_ * 128:(j_ + 1) * 128],
                                      (vtb,), (payVh[h_][kt],))
                    if stop != "Ma0":
                        for h_ in range(H):
                            P.collective("AllGather", groups, payK[h_].ap().opt(), agK[h_].ap().opt(),
                                         (payKn[h_], payKr[h_]), (agKb[h_],), cc_inc)
                            P.collective("AllGather", groups, payV[h_].ap().opt(), agV[h_].ap().opt(),
                                         tuple(payVh[h_]), (agVb[h_],), cc_inc)
                    P.barrier()
                if stop in ("Ma0", "Ma"):
                    return "stopped"

                with ExitStack() as ph:
                    usr = Ring(ph, nc, "usr", 2, [128, T], F32)
                    ztr = Ring(ph, nc, "ztr", 2, [128, T + 2], F32)
                    tcr = Ring(ph, nc, "tcr", 2, [128, T], F32)
                    ssr = Ring(ph, nc, "ssr", 2, [128, T], BF16)
                    for (zt, zb) in zip(ztr.t, ztr.b):
                        P.op("pool", lambda e, zt=zt: e.memset(zt[:, 0:2], 0.0), (), (zb,))
                    for j in range(CC):
                        bu = list(range(0, NH))
                        bc = list(range(NH, 2 * NH))
                        bg = [4 + (j % 2) * NH + hh for hh in range(NH)]
                        wt, wb = next_w()
                        proj(wt, wb, slice(0, 128), MC, hsrc, hb, bu)
                        ut, ub = usr.next()
                        for hh in range(NH):
                            P.op("act", lambda e, ut=ut, hh=hh: e.activation(ut[:, hh * TT:(hh + 1) * TT],
                                                                           banks[bu[hh]][:, 0:TT], AF.Copy),
                                 (bb[bu[hh]],), (ub,))
                        wt, wb = next_w()
                        proj(wt, wb, slice(0, 128), MC, hsrc, hb, bc)
                        zt, zb = ztr.next()
                        for hh in range(NH):
                            P.op("dve", lambda e, zt=zt, ut=ut, hh=hh: e.tensor_tensor(
                                zt[:, 2 + hh * TT:2 + (hh + 1) * TT], ut[:, hh * TT:(hh + 1) * TT],
                                banks[bc[hh]][:, 0:TT], ALU.mult), (ub, bb[bc[hh]]), (zb,))
                        wt, wb = next_w()
                        proj(wt, wb, slice(0, 128), MC, hsrc, hb, bg)
                        P.op("pool", lambda e, zt=zt, j=j: e.tensor_copy(zl[:, j, :], zt[:, T:T + 2]), (zb,), (zlb,))
                        tt, tb = tcr.next()
                        P.op("act", lambda e, tt=tt, zt=zt, j=j: e.activation(tt[:, :], zt[:, 2:2 + T], AF.Identity,
                                                                             bias=V("cb", j), scale=V("cw2", j)),
                             (zb, vb), (tb,))
                        P.op("dve", lambda e, tt=tt, zt=zt, j=j: e.scalar_tensor_tensor(
                            tt[:, :], zt[:, 1:1 + T], V("cw1", j), tt[:, :], ALU.mult, ALU.add), (zb, vb, tb), (tb,))
                        P.op("dve", lambda e, tt=tt, zt=zt, j=j: e.scalar_tensor_tensor(
                            tt[:, :], zt[:, 0:T], V("cw0", j), tt[:, :], ALU.mult, ALU.add), (zb, vb, tb), (tb,))
                        P.op("pool", lambda e, tt=tt, j=j: e.tensor_copy(t2[:, j, :], tt[:, 0:2]), (tb,), (t2b,))
                        P.op("act", lambda e, j=j: e.activation(gb2[:, j, :], banks[bg[0]][:, 0:2], AF.Copy),
                             (bb[bg[0]],), (gb2b,))
                        st, sb_ = ssr.next()
                        for hh in range(NH):
                            P.op("dve", lambda e, st=st, tt=tt, hh=hh: e.tensor_tensor(
                                st[:, hh * TT:(hh + 1) * TT], tt[:, hh * TT:(hh + 1) * TT], banks[bg[hh]][:, 0:TT],
                                ALU.mult), (tb, bb[bg[hh]]), (sb_,))
                        P.dma("sp", sds[j], st[:, :], (sb_,), (sdb[j],))
                    P.dma("sp", payZ.ap(), zl[:, :, :].rearrange("p c t -> p (c t)"), (zlb,), (payZb,))
                    P.collective("AllGather", groups, payZ.ap().opt(), agZ.ap().opt(), (payZb,), (agZb,), cc_inc)
                    P.barrier()
                if stop == "Mb":
                    return "stopped"

                ast = ExitStack()
                mx.enter_context(ast)
                attn = sb(ast, "attn", [128, H, T], BF16)
                attnb = [Buf("attn%d" % h) for h in range(H)]
                qst = ExitStack()
                mx.enter_context(qst)
                qln = sb(qst, "qln", [128, QC, T], BF16)
                qlnb = [Buf("qln%d" % i) for i in range(QC)]
                with ExitStack() as ph:
                    qraw = sb(ph, "qraw", [128, QC, T], F32)
                    qrawb = [Buf("qraw%d" % i) for i in range(QC)]
                    sqr = Ring(ph, nc, "sqr2", 2, [128, T], BF16)
                    ssb = [6, 7][:NH]
                    for kc in range(QC):
                        wt, wb = next_w()
                        bs = [(kc % 2) * NH + hh for hh in range(NH)]
                        proj(wt, wb, slice(0, 128), MC, hsrc, hb, bs)
                        st, sb_ = sqr.next()
                        for hh in range(NH):
                            P.op("act", lambda e, st=st, hh=hh, bi=bs[hh]: e.activation(
                                st[:, hh * TT:(hh + 1) * TT], banks[bi][:, 0:TT], AF.Square), (bb[bs[hh]],), (sb_,))
                            P.op("dve", lambda e, hh=hh, bi=bs[hh], kc=kc: e.tensor_copy(
                                qraw[:, kc, hh * TT:(hh + 1) * TT], banks[bi][:, 0:TT]), (bb[bs[hh]],), (qrawb[kc],))
                        for hh in range(NH):
                            mm(banks[ssb[hh]][:, 0:TT], ones[:, :], st[:, hh * TT:(hh + 1) * TT], kc == 0, kc == QC - 1,
                               (onesb, sb_), (bb[ssb[hh]],))
                    rstd_from(ssb, c.QL, rstd, rstdb)
                    for kc in range(QC):
                        P.op("dve", lambda e, kc=kc: e.scalar_tensor_tensor(
                            qln[:, kc, :], qraw[:, kc, :], V("qln", kc), rstd[:, :], ALU.mult, ALU.mult),
                            (qrawb[kc], vb, rstdb), (qlnb[kc],))
                    P.barrier()
                if stop == "Mc":
                    return "stopped"

                scale = 192.0 ** -0.5
                with ExitStack() as ph:
                    wuqr = Ring(ph, nc, "wuqr", 2, [128, QC, 256], BF16)
                    sqr = Ring(ph, nc, "sqr3", 1, [128, T], BF16)
                    sqr64 = Ring(ph, nc, "sqr64", 1, [64, T], BF16)
                    rsr = Ring(ph, nc, "rsr2", 1, [128, T], F32)
                    qnr = Ring(ph, nc, "qnr", 2, [128, T], BF16)
                    qrr = Ring(ph, nc, "qrr", 2, [64, T], BF16)
                    q1 = sb(ph, "q1", [64, T], F32)
                    q2 = sb(ph, "q2", [64, T], F32)
                    q1b, q2b = Buf("q1"), Buf("q2")
                    knr = Ring(ph, nc, "aknr", 2, [128, T], BF16)
                    krr = Ring(ph, nc, "akrr", 2, [64, T], BF16)
                    vvr = Ring(ph, nc, "avvr", 2, [128, KT, 128], BF16)
                    pr = Ring(ph, nc, "apr", 4, [128, TT], BF16)
                    rcr = Ring(ph, nc, "arc", 2, [128, TT], F32)
                    qsl = lambda kc, hh: qln[:, kc, hh * TT:(hh + 1) * TT]
                    for h in range(H):
                        wt, wb = wuqr.next()
                        P.dma("pool", wt[:, :, :], wuq[h].rearrange("p (k n) -> p k n", n=256), (), (wb,))
                        bn = list(range(0, NH))
                        br = list(range(NH, 2 * NH))
                        bw = list(range(2 * NH, 3 * NH))
                        bs_ = list(range(3 * NH, 4 * NH))
                        proj(wt, wb, slice(0, 128), QC, qsl, qlnb, bn)
                        proj(wt, wb, slice(128, 192), QC, qsl, qlnb, br, M=64)
                        proj(wt, wb, slice(192, 256), QC, qsl, qlnb, bw, M=64)
                        st, sb_ = sqr.next()
                        s6, s6b = sqr64.next()
                        for hh in range(NH):
                            hs = slice(hh * TT, (hh + 1) * TT)
                            P.op("act", lambda e, st=st, hs=hs, bi=bn[hh]: e.activation(st[:, hs], banks[bi][:, 0:TT], AF.Square),
                                 (bb[bn[hh]],), (sb_,))
                            P.op("act", lambda e, s6=s6, hs=hs, bi=br[hh]: e.activation(s6[:, hs], banks[bi][0:64, 0:TT], AF.Square),
                                 (bb[br[hh]],), (s6b,))
                        for hh in range(NH):
                            hs = slice(hh * TT, (hh + 1) * TT)
                            mm(banks[bs_[hh]][:, 0:TT], ones[:, :], st[:, hs], True, False, (onesb, sb_), (bb[bs_[hh]],))
                            mm(banks[bs_[hh]][:, 0:TT], ones[0:64, :], s6[:, hs], False, True, (onesb, s6b), (bb[bs_[hh]],))
                        rt, rb = rsr.next()
                        for hh in range(NH):
                            hs = slice(hh * TT, (hh + 1) * TT)
                            P.op("dve", lambda e, hs=hs, bi=br[hh]: e.tensor_tensor(q1[:, hs], banks[bi][0:64, 0:TT], Cq[:, hs], ALU.mult),
                                 (bb[br[hh]], Cqb), (q1b,))
                            P.op("dve", lambda e, hs=hs, bi=bw[hh]: e.tensor_tensor(q2[:, hs], banks[bi][0:64, 0:TT], Sq[:, hs], ALU.mult),
                                 (bb[bw[hh]], Sqb), (q2b,))
                        P.op("dve", lambda e: e.tensor_tensor(q1[:, :], q1[:, :], q2[:, :], ALU.add), (q1b, q2b), (q1b,))
                        rstd_from(bs_, 192, rt, rb)
                        qn, qnb = qnr.next()
                        qr, qrb = qrr.next()
                        for hh in range(NH):
                            hs = slice(hh * TT, (hh + 1) * TT)
                            P.op("dve", lambda e, qn=qn, rt=rt, hs=hs, bi=bn[hh]: e.scalar_tensor_tensor(
                                qn[:, hs], banks[bi][:, 0:TT], V("gq_n"), rt[:, hs], ALU.mult, ALU.mult),
                                (bb[bn[hh]], vb, rb), (qnb,))
                        P.op("dve", lambda e, qr=qr, rt=rt: e.tensor_tensor(qr[:, :], q1[:, :], rt[0:64, :], ALU.mult),
                             (q1b, rb), (qrb,))
                        ob = [4 + 2 * hh for hh in range(NH)] if NH <= 2 else None
                        lb = [5 + 2 * hh for hh in range(NH)]
                        started = [False] * NH
                        scnt = 0
                        pending = []

                        def emit_pv(tk):
                            (hh_, kt_i, c0_, pt_, pb_, vv_, vvb_, st_, last_) = tk
                            mm(banks[ob[hh_]][:, c0_:TT], vv_[:, kt_i, :], pt_[:, c0_:TT], st_, last_,
                               (vvb_, pb_), (bb[ob[hh_]],))
                            mm(banks[lb[hh_]][:, c0_:TT], ones[:, :], pt_[:, c0_:TT], st_, last_,
                               (onesb, pb_), (bb[lb[hh_]],))
                        blocks = [("ag", r) for r in range(R - 1)] + [("own", 0)]
                        for bi_blk, (kind, r) in enumerate(blocks):
                            last_blk = bi_blk == len(blocks) - 1
                            kn_, knb_ = knr.next()
                            kr_, krb_ = krr.next()
                            vv, vvb = vvr.next()
                            if kind == "ag":
                                aK, aV = agK[h].ap(), agV[h].ap()
                                P.dma("sp", kn_[:, :], aK[r * 192:r * 192 + 128, :], (agKb[h],), (knb_,))
                                P.dma("sp", kr_[:, :], aK[r * 192 + 128:(r + 1) * 192, :], (agKb[h],), (krb_,))
                                P.dma("sp", vv[:, :, :], aV[r * T:(r + 1) * T, :].rearrange("(k p) d -> p k d", p=128),
                                      (agVb[h],), (vvb,))
                            else:
                                P.dma("sp", kn_[:, :], payK[h].ap()[0:128, :], (payKn[h],), (knb_,))
                                P.dma("sp", kr_[:, :], payK[h].ap()[128:192, :], (payKr[h],), (krb_,))
                                P.dma("sp", vv[:, :, :], payV[h].ap().rearrange("(k p) d -> p k d", p=128),
                                      tuple(payVh[h]), (vvb,))
                            for kt in range(KT):
                                for hh in range(NH):
                                    c0 = 0
                                    diag = False
                                    if kind == "own":
                                        if kt * 128 >= (hh + 1) * TT:
                                            continue
                                        if kt * 128 >= hh * TT:
                                            diag = True
                                            c0 = kt * 128 - hh * TT
                                    si = scnt % 4
                                    scnt += 1
                                    qs = slice(hh * TT + c0, (hh + 1) * TT)
                                    ks = slice(kt * 128, (kt + 1) * 128)
                                    mm(banks[si][:, c0:TT], kn_[:, ks], qn[:, qs], True, False, (knb_, qnb), (bb[si],))
                                    mm(banks[si][:, c0:TT], kr_[:, ks], qr[:, qs], False, True, (krb_, qrb), (bb[si],))
                                    pt, pb = pr.next()
                                    if kind == "ag":
                                        P.op("act", lambda e, pt=pt, si=si, r=r: e.activation(
                                            pt[:, 0:TT], banks[si][:, 0:TT], AF.Exp, bias=mskt[:, r:r + 1], scale=scale),
                                            (bb[si], mskb_), (pb,))
                                    else:
                                        P.op("act", lambda e, pt=pt, si=si, c0=c0: e.activation(
                                            pt[:, c0:TT], banks[si][:, c0:TT], AF.Exp, scale=scale), (bb[si],), (pb,))
                                        if diag:
                                            P.op("dve", lambda e, pt=pt, c0=c0: e.memset(pt[64:128, c0:c0 + 64], 0.0), (), (pb,))
                                    is_last = last_blk and (kt + 1) * 128 >= (hh + 1) * TT
                                    pending.append((hh, kt, c0, pt, pb, vv, vvb, not started[hh], is_last))
                                    started[hh] = True
                                    if len(pending) > 2:
                                        emit_pv(pending.pop(0))
                        while pending:
                            emit_pv(pending.pop(0))
                        for hh in range(NH):
                            rc, rcb = rcr.next()
                            P.op("act", lambda e, rc=rc, hh=hh: e.activation(rc[:, :], banks[lb[hh]][:, 0:TT], AF.Ln), (bb[lb[hh]],), (rcb,))
                            P.op("act", lambda e, rc=rc: e.activation(rc[:, :], rc[:, :], AF.Exp, scale=-1.0), (rcb,), (rcb,))
                            P.op("dve", lambda e, rc=rc, hh=hh, h=h: e.tensor_tensor(
                                attn[:, h, hh * TT:(hh + 1) * TT], banks[ob[hh]][:, 0:TT], rc[:, :], ALU.mult),
                                (bb[ob[hh]], rcb), (attnb[h],))
                    P.barrier()
                qst.close()
                if stop == "Md":
                    return "stopped"

                with ExitStack() as ph:
                    s_sb = sb(ph, "s_sb", [128, CC, T], BF16)
                    ssb_ = [Buf("s%d" % j) for j in range(CC)]
                    for j in range(CC):
                        P.dma("sp", s_sb[:, j, :], sds[j], (sdb[j],), (ssb_[j],))
                    zall = sb(ph, "zall", [128, R, CC * 2], F32)
                    zallb = Buf("zall")
                    P.dma("sp", zall[:, :, :], agZ.ap().rearrange("(r p) c -> p r c", p=128), (agZb,), (zallb,))
                    P.op("dve", lambda e: e.tensor_scalar(zprev[:, :, :].rearrange("p c t -> p (c t)"), zall[:, 0, :],
                                                         selt[:, 0:1], None, ALU.mult), (zallb, selb), (zpb,))
                    for r in range(1, R):
                        P.op("dve", lambda e, r=r: e.scalar_tensor_tensor(
                            zprev[:, :, :].rearrange("p c t -> p (c t)"), zall[:, r, :], selt[:, r:r + 1],
                            zprev[:, :, :].rearrange("p c t -> p (c t)"), ALU.mult, ALU.add), (zallb, selb, zpb), (zpb,))
                    o0, o1 = c.vcols["cw0"][0], c.vcols["cw1"][0]
                    fx = sb(ph, "fx", [128, CC, 2], F32)
                    fxb = Buf("fx")
                    P.op("dve", lambda e: e.tensor_tensor(fx[:, :, 0], zprev[:, :, 0], vecs[:, o0:o0 + CC], ALU.mult), (zpb, vb), (fxb,))
                    P.op("dve", lambda e: e.tensor_tensor(fx[:, :, 1], zprev[:, :, 1], vecs[:, o1:o1 + CC], ALU.mult), (zpb, vb), (fxb,))
                    P.op("dve", lambda e: e.tensor_tensor(t2[:, :, 0], t2[:, :, 0], fx[:, :, 0], ALU.add), (t2b, fxb), (t2b,))
                    P.op("dve", lambda e: e.tensor_tensor(t2[:, :, 0], t2[:, :, 0], fx[:, :, 1], ALU.add), (t2b, fxb), (t2b,))
                    P.op("dve", lambda e: e.tensor_tensor(fx[:, :, 0], zprev[:, :, 1], vecs[:, o0:o0 + CC], ALU.mult), (zpb, vb), (fxb,))
                    P.op("dve", lambda e: e.tensor_tensor(t2[:, :, 1], t2[:, :, 1], fx[:, :, 0], ALU.add), (t2b, fxb), (t2b,))
                    P.op("dve", lambda e: e.tensor_tensor(s_sb[:, :, 0:2], t2[:, :, :], gb2[:, :, :], ALU.mult),
                         (t2b, gb2b), tuple(ssb_))
                    wcor = Ring(ph, nc, "wcor", 2, [128, CC, 128], BF16)
                    wmor = Ring(ph, nc, "wmor", 2, [128, H, 128], BF16)
                    gtr = Ring(ph, nc, "gtr", 2, [128, TT], F32)
                    tar = Ring(ph, nc, "tar", 2, [128, T], F32)
                    tbr = Ring(ph, nc, "tbr", 2, [128, TT], F32)
                    mstr = Ring(ph, nc, "mstr", 2, [128, T], BF16)
                    obg0, obg1 = c.vcols["bg0"][0], c.vcols["bg1"][0]
                    for m in range(MC):
                        bgl = list(range(0, NH))
                        byc = list(range(NH, 2 * NH))
                        wt, wb = next_w()
                        proj(wt, wb, slice(0, 128), MC, hsrc, hb, bgl)
                        wt, wb = wcor.next()
                        P.dma("pool", wt[:, :, :], wco[m].rearrange("p (k n) -> p k n", n=128), (), (wb,))
                        proj(wt, wb, slice(0, 128), CC, lambda kc, hh: s_sb[:, kc, hh * TT:(hh + 1) * TT], ssb_, byc)
                        ta, tab = tar.next()
                        for hh in range(NH):
                            gt, gtb = gtr.next()
                            P.op("act", lambda e, gt=gt, hh=hh, m=m: e.activation(
                                gt[:, :], banks[bgl[hh]][:, 0:TT], AF.Sigmoid, bias=vecs[:, obg0 + m:obg0 + m + 1]),
                                (bb[bgl[hh]], vb), (gtb,))
                            P.op("dve", lambda e, gt=gt, ta=ta, hh=hh: e.tensor_tensor(
                                ta[:, hh * TT:(hh + 1) * TT], gt[:, :], banks[byc[hh]][:, 0:TT], ALU.mult),
                                (gtb, bb[byc[hh]]), (tab,))
                        bgl = list(range(4, 4 + NH))
                        byc = list(range(4 + NH, 4 + 2 * NH))
                        wt, wb = next_w()
                        proj(wt, wb, slice(0, 128), MC, hsrc, hb, bgl)
                        wt, wb = wmor.next()
                        P.dma("pool", wt[:, :, :], wmo[m].rearrange("p (k n) -> p k n", n=128), (), (wb,))
                        proj(wt, wb, slice(0, 128), H, lambda kc, hh: attn[:, kc, hh * TT:(hh + 1) * TT], attnb, byc)
                        ms, msb = mstr.next()
                        for hh in range(NH):
                            gt, gtb = gtr.next()
                            tb_, tbb = tbr.next()
                            P.op("act", lambda e, gt=gt, hh=hh, m=m: e.activation(
                                gt[:, :], banks[bgl[hh]][:, 0:TT], AF.Sigmoid, bias=vecs[:, obg1 + m:obg1 + m + 1]),
                                (bb[bgl[hh]], vb), (gtb,))
                            P.op("dve", lambda e, gt=gt, tb_=tb_, hh=hh: e.tensor_tensor(
                                tb_[:, :], gt[:, :], banks[byc[hh]][:, 0:TT], ALU.mult), (gtb, bb[byc[hh]]), (tbb,))
                            P.op("dve", lambda e, tb_=tb_, ta=ta, ms=ms, hh=hh: e.tensor_tensor(
                                ms[:, hh * TT:(hh + 1) * TT], tb_[:, :], ta[:, hh * TT:(hh + 1) * TT], ALU.add),
                                (tbb, tab), (msb,))
                        P.dma("sp", mgs[m], ms[:, :], (msb,), (mgb[m],))
                    P.barrier()
                ast.close()
                wring_st.close()
                if stop == "Me":
                    return "stopped"

                with ExitStack() as ph:
                    for m in range(MC):
                        P.dma("sp", hbuf[:, m, :], mgs[m], (mgb[m],), (hb[m],))
                    wor = Ring(ph, nc, "wor", 3, [128, MC, 128], BF16)
                    xin = Ring(ph, nc, "oxin", 2, [128, T], F32)
                    xo = Ring(ph, nc, "oxo", 2, [128, T], F32)
                    for o in range(MC):
                        wt, wb = wor.next()
                        P.dma("pool", wt[:, :, :], wo[o].rearrange("p (k n) -> p k n", n=128), (), (wb,))
                        bs = [(o % 4) * NH + hh for hh in range(NH)] if NH <= 2 else None
                        proj(wt, wb, slice(0, 128), MC, hsrc, hb, bs)
                        resid_evac((xin, xo), bs, o, mod(5, o), (modb,), xw, xw)
                    P.barrier()

        stages = [("norm1", lambda: norm_phase(xT, 0, 0, mid=phase0)), ("ffn1", lambda: ffn_phase(0, 3, xT, xw)),
                  ("norm2", lambda: norm_phase(xw, 1, 3)), ("mixer", lambda: mixer_phase()),
                  ("norm3", lambda: norm_phase(xw, 2, 6)), ("ffn2", lambda: ffn_phase(1, 4, xw, out))]
        done = None
        for name, fn in stages:
            r_ = fn()
            done = name
            if stop is not None and (stop == name or r_ == "stopped"):
                break
        if done != "ffn2":
            srcx = xT if done == "norm1" else xw
            for m in range(MC):
                P.dma("sp", out[m], srcx[m], (xdb[m],), (xdb[m],))
        P.barrier()
    return nc


def _swap_idx():
    return np.concatenate([np.arange(32, 64), np.arange(0, 32)])


def prep_weights(c, inp):
    D, MC, FC, CC, QC, KC, H, HG, NHG, CW, QL, KVL = c.D, c.MC, c.FC, c.CC, c.QC, c.KC, c.H, c.HG, c.NHG, c.CW, c.QL, c.KVL
    f = lambda a: np.ascontiguousarray(a, dtype=np.float32)
    w = {}
    wa = np.asarray(inp["w_ada"][0])
    wal = np.zeros((c.R * c.NCC, 128, MC * 128), np.float32)
    wal[:9 * MC] = wa.reshape(MC, 128, 9 * MC, 128).transpose(2, 1, 0, 3).reshape(9 * MC, 128, MC * 128)
    w["wada_all"] = wal
    KH = MC // 2
    for l, pre in ((1, "ffn1"), (2, "ffn2")):
        wg = np.asarray(inp[pre + "_w_gate"][0]).reshape(2, KH, 128, FC, 128)
        wu = np.asarray(inp[pre + "_w_up"][0]).reshape(2, KH, 128, FC, 128)
        st = np.stack([wg, wu], axis=0)
        w["wgu%d" % l] = f(st.transpose(4, 1, 3, 2, 0, 5).reshape(FC * 2, 128, KH * 2 * 128))
        wd = np.asarray(inp[pre + "_w_down"][0])
        for g in range(c.G):
            blk = wd[c.FB[g] * 128:(c.FB[g] + c.FG[g]) * 128].reshape(c.FG[g], 128, MC // 2, 256)
            w["wdn%d_%d" % (l, g)] = f(blk.transpose(2, 1, 0, 3).reshape(MC // 2, 128, c.FG[g] * 256))
    wi = np.asarray(inp["w_in"][0])
    o_u, o_gb, o_gc = 0, CW, 2 * CW
    o_q = 3 * CW
    o_kv = o_q + QL
    o_pe = o_kv + KVL
    o_g = o_pe + 64
    sw = _swap_idx()
    cols = []
    for kc in range(KC):
        cols.append(o_kv + kc * 128 + np.arange(128))
    cols.append(np.concatenate([o_pe + np.arange(64), o_pe + sw]))
    for j in range(CC):
        cols.append(o_u + j * 128 + np.arange(128))
        cols.append(o_gc + j * 128 + np.arange(128))
        cols.append(o_gb + j * 128 + np.arange(128))
    for kc in range(QC):
        cols.append(o_q + kc * 128 + np.arange(128))
    for m in range(MC):
        cols.append(o_g + m * 128 + np.arange(128))
        cols.append(o_g + D + m * 128 + np.arange(128))
    allc = np.concatenate(cols)
    assert len(cols) == c.NWIN
    w["win"] = f(wi[:, allc].reshape(MC, 128, c.NWIN, 128).transpose(2, 1, 0, 3).reshape(c.NWIN, 128, MC * 128))
    w["wco"] = f(np.asarray(inp["w_conv_out"][0]).reshape(CC, 128, MC, 128).transpose(2, 1, 0, 3).reshape(MC, 128, CC * 128))
    w["wmo"] = f(np.asarray(inp["w_mla_out"][0]).reshape(H, 128, MC, 128).transpose(2, 1, 0, 3).reshape(MC, 128, H * 128))
    w["wo"] = f(np.asarray(inp["w_out"][0]).reshape(MC, 128, MC, 128).transpose(2, 1, 0, 3).reshape(MC, 128, MC * 128))
    wq = np.asarray(inp["w_uq"][0])
    qc = np.concatenate([h * 192 + np.concatenate([np.arange(128), 128 + np.arange(64), 128 + sw]) for h in range(H)])
    w["wuq"] = f(wq[:, qc].reshape(QC, 128, H, 256).transpose(2, 1, 0, 3).reshape(H, 128, QC * 256))
    wkv = np.asarray(inp["w_ukv"][0])
    kcols = np.concatenate([h * 256 + np.arange(128) for h in range(H)])
    vcols = np.concatenate([h * 256 + 128 + np.arange(128) for h in range(H)])
    w["wuk"] = f(wkv[:, kcols].reshape(KC, 128, H, 128).transpose(2, 1, 0, 3).reshape(H, 128, KC * 128))
    w["wuv"] = f(wkv[:, vcols].reshape(KC, 128, NHG, HG * 128).transpose(2, 1, 0, 3).reshape(NHG, 128, KC * HG * 128))
    vec = np.zeros((128, c.NV), np.float32)

    def put(name, arr):
        o, wd_ = c.vcols[name]
        a = np.asarray(arr, np.float32).reshape(wd_, 128).T
        vec[:, o:o + wd_] = a
    put("gain1", inp["norm_ffn1"][0])
    put("gain2", inp["norm_mix"][0])
    put("gain3", inp["norm_ffn2"][0])
    bpad = np.zeros(c.R * c.NCC * 128, np.float32)
    bpad[:9 * D] = np.asarray(inp["b_ada"][0], np.float32)
    put("bada", bpad)
    bgt = np.asarray(inp["b_gate"][0])
    put("bg0", bgt[:D])
    put("bg1", bgt[D:])
    cw = np.asarray(inp["conv_w"][0])
    put("cw0", cw[0])
    put("cw1", cw[1])
    put("cw2", cw[2])
    put("cb", inp["conv_b"][0])
    put("qln", inp["q_lat_norm"][0])
    put("kvn", inp["kv_lat_norm"][0])
    qn = np.asarray(inp["q_norm"][0], np.float32)
    kn = np.asarray(inp["k_norm"][0], np.float32)
    for pre, g in (("gq", qn), ("gk", kn)):
        vec[:, c.vcols[pre + "_n"][0]] = g[:128]
        vec[:64, c.vcols[pre + "_r"][0]] = g[128:]
        vec[:64, c.vcols[pre + "_rs"][0]] = g[128:][sw]
    inv = (np.float32(10000.0) ** (-np.arange(0, 64, 2, dtype=np.float32) / np.float32(64))).astype(np.float32)
    vec[:64, c.vcols["freq"][0]] = np.concatenate([inv, inv])
    vec[:32, c.vcols["sgn"][0]] = -1.0
    vec[32:64, c.vcols["sgn"][0]] = 1.0
    vec[:, c.vcols["halfpi"][0]] = 0.5 * math.pi
    w["vecs"] = vec
    return w


def prep_core_inputs(c, inp, w):
    x = np.asarray(inp["x"], np.float32)
    cc = np.asarray(inp["c"], np.float32)
    pos = np.asarray(inp["positions"], np.int32)
    maps = []
    for core in range(c.NCORES):
        b, r = core // c.R, core % c.R
        t0 = r * c.T
        m = {k_: v_ for k_, v_ in w.items() if k_ != "wada_all"}
        m["wada"] = w["wada_all"][r * c.NCC:(r + 1) * c.NCC]
        m["xT"] = np.ascontiguousarray(x[b, t0:t0 + c.T, :].T).reshape(c.MC, 128, c.T)
        m["cT"] = np.ascontiguousarray(cc[b].reshape(c.MC, 128).T)
        m["posb"] = np.ascontiguousarray(np.broadcast_to(pos[b, t0:t0 + c.T][None, :], (64, c.T)))
        mk = np.zeros((128, max(c.R - 1, 1)), np.float32)
        for rr in range(c.R - 1):
            mk[:, rr] = 0.0 if rr < r else NEG
        m["maskb"] = mk
        sl = np.zeros((128, c.R), np.float32)
        if r > 0:
            sl[:, r - 1] = 1.0
        m["sel"] = sl
        maps.append(m)
    return maps


_CACHE = {}


def run(c, inp, stop=None):
    key = (c.D, c.F, c.S, c.H, stop)
    if key not in _CACHE:
        _CACHE[key] = build_program(c, stop=stop)
    nc = _CACHE[key]
    w = prep_weights(c, inp)
    maps = prep_core_inputs(c, inp, w)
    res = run_bass_kernel_spmd(nc, maps, core_ids=list(range(c.NCORES)))
    outp = np.empty((c.B, c.S, c.D), np.float32)
    for core in range(c.NCORES):
        b, r = core // c.R, core % c.R
        o = np.asarray(res.results[core]["out"]).reshape(c.D, c.T)
        outp[b, r * c.T:(r + 1) * c.T, :] = o.T
    return outp


def kernel(**inputs):
    return run(Cfg(), inputs)
```
